# Optimizing a Trainium2 kernel written in Bass

```python
import math
import jax
import jax.numpy as jnp
from jax import lax
import numpy as np

D_MODEL = 2048
BATCH = 8
SEQ = 2048
DEPTH = 4

N_EVEN = (DEPTH + 1) // 2
N_ODD = DEPTH // 2
D_FF = ((8 * D_MODEL // 3 + 255) // 256) * 256
MIX_WIDTH = D_MODEL
POOL_WIDTH = MIX_WIDTH // 2
POOL_WINDOWS = (2, 4, 8, 16)
POOL_GROUPS = len(POOL_WINDOWS)
POOL_GROUP_WIDTH = POOL_WIDTH // POOL_GROUPS
DIFF_HEAD_DIM = 128
DIFF_V_DIM = 2 * DIFF_HEAD_DIM
ATTN_WIDTH = MIX_WIDTH - POOL_WIDTH
N_HEADS_DIFF = ATTN_WIDTH // DIFF_V_DIM
AB_IN_WIDTH = POOL_WIDTH + 3 * ATTN_WIDTH
CONV_WIDTH = 3
REL_BUCKETS = 32
REL_MAX_DIST = 128
Q_BLOCK = 128
NORM_EPS = 1e-6
NEG_INF = -1e30

kernel_name = "hybrid_pool_diffattn_shortconv_macaron"


def _rmsnorm(x, g):
    xf = x.astype(jnp.float32)
    xf = xf * lax.rsqrt(jnp.mean(xf * xf, axis=-1, keepdims=True) + NORM_EPS)
    return (xf * g.astype(jnp.float32)).astype(x.dtype)


def _swiglu(h, w_gate, w_up, w_down):
    return (jax.nn.silu(h @ w_gate) * (h @ w_up)) @ w_down


def _t5_bucket(n):
    max_exact = REL_BUCKETS // 2
    nf = jnp.maximum(n, max_exact).astype(jnp.float32)
    large = max_exact + (jnp.log(nf / max_exact) / math.log(REL_MAX_DIST / max_exact)
                         * (REL_BUCKETS - max_exact)).astype(jnp.int32)
    large = jnp.minimum(large, REL_BUCKETS - 1)
    return jnp.where(n < max_exact, n, large)


def _pool_mixer(u, w_groups, scale):
    s = u.shape[1]
    uf = u.astype(jnp.float32)
    cs = jnp.cumsum(uf, axis=1)
    t = jnp.arange(s)
    outs = []
    for g, w in enumerate(POOL_WINDOWS):
        sl = slice(g * POOL_GROUP_WIDTH, (g + 1) * POOL_GROUP_WIDTH)
        c = cs[..., sl]
        prev = jnp.pad(c, ((0, 0), (w, 0), (0, 0)))[:, :s]
        cnt = jnp.minimum(t + 1, w).astype(jnp.float32)[None, :, None]
        p = ((c - prev) / cnt - uf[..., sl]).astype(u.dtype)
        outs.append(p @ w_groups[g])
    return jnp.concatenate(outs, axis=-1) * scale


def _diff_attention(q, k, v, lam, rel_bias):
    s = q.shape[3]
    outs = []
    for i in range(s // Q_BLOCK):
        s0, e = i * Q_BLOCK, (i + 1) * Q_BLOCK
        qb = q[:, :, :, s0:e]
        kb = k[:, :, :, :e]
        vb = v[:, :, :e]
        scores = jnp.einsum('bhmqd,bhmkd->bhmqk', qb, kb).astype(jnp.float32)
        qpos = s0 + jnp.arange(Q_BLOCK)
        kpos = jnp.arange(e)
        dist = qpos[:, None] - kpos[None, :]
        bias = rel_bias.astype(jnp.float32)[_t5_bucket(jnp.maximum(dist, 0))]
        scores = scores + jnp.transpose(bias, (2, 0, 1))[None, :, None]
        scores = jnp.where((dist >= 0)[None, None, None], scores, NEG_INF)
        p = jax.nn.softmax(scores, axis=-1)
        a = p[:, :, 0] - lam * p[:, :, 1]
        outs.append(jnp.einsum('bhqk,bhkv->bhqv', a.astype(vb.dtype), vb))
    return jnp.concatenate(outs, axis=2)


def _even_mixer(h, layer_idx, w_in, pool_w, pool_scale, lq1, lk1, lq2, lk2, subln_g, w_out, rel_bias):
    b, s, _ = h.shape
    z = h @ w_in
    u = z[..., :POOL_WIDTH]
    o0 = POOL_WIDTH
    q = z[..., o0:o0 + ATTN_WIDTH].reshape(b, s, N_HEADS_DIFF, 2, DIFF_HEAD_DIM)
    k = z[..., o0 + ATTN_WIDTH:o0 + 2 * ATTN_WIDTH].reshape(b, s, N_HEADS_DIFF, 2, DIFF_HEAD_DIM)
    v = z[..., o0 + 2 * ATTN_WIDTH:].reshape(b, s, N_HEADS_DIFF, DIFF_V_DIM)
    q = jnp.transpose(q, (0, 2, 3, 1, 4)) * (DIFF_HEAD_DIM ** -0.5)
    k = jnp.transpose(k, (0, 2, 3, 1, 4))
    v = jnp.transpose(v, (0, 2, 1, 3))
    lam_init = 0.8 - 0.6 * math.exp(-0.3 * layer_idx)
    f32 = jnp.float32
    lam = (jnp.exp(jnp.sum(lq1.astype(f32) * lk1.astype(f32)))
           - jnp.exp(jnp.sum(lq2.astype(f32) * lk2.astype(f32))) + lam_init)
    att = _diff_attention(q, k, v, lam, rel_bias)
    att = _rmsnorm(att, subln_g) * (1.0 - lam_init)
    att = jnp.transpose(att, (0, 2, 1, 3)).reshape(b, s, ATTN_WIDTH)
    pool = _pool_mixer(u, pool_w, pool_scale)
    return jnp.concatenate([pool, att], axis=-1) @ w_out


def _conv_mixer(h, w_in, conv_w, w_out):
    d = h.shape[-1]
    z = h @ w_in
    gate_b, gate_c, xin = z[..., :d], z[..., d:2 * d], z[..., 2 * d:]
    u = gate_c * xin
    y = lax.conv_general_dilated(u, conv_w[:, None, :].astype(u.dtype), window_strides=(1,),
                                 padding=[(CONV_WIDTH - 1, 0)],
                                 dimension_numbers=('NWC', 'WIO', 'NWC'),
                                 feature_group_count=d)
    return (gate_b * y) @ w_out


def setup_inputs(seed: int = 0) -> dict:
    key = jax.random.key(seed)
    ks = jax.random.split(key, 24)
    f32 = jnp.float32
    nrm = lambda k, shape, s: jax.random.normal(k, shape, f32) * s
    return {
        "x": nrm(ks[0], (BATCH, SEQ, D_MODEL), 1.0),
        "ffn_norm_g": 1.0 + nrm(ks[1], (DEPTH, 2, D_MODEL), 0.02),
        "mix_norm_g": 1.0 + nrm(ks[2], (DEPTH, D_MODEL), 0.02),
        "final_norm_g": 1.0 + nrm(ks[3], (D_MODEL,), 0.02),
        "ffn_w_gate": nrm(ks[4], (DEPTH, 2, D_MODEL, D_FF), D_MODEL ** -0.5),
        "ffn_w_up": nrm(ks[5], (DEPTH, 2, D_MODEL, D_FF), D_MODEL ** -0.5),
        "ffn_w_down": nrm(ks[6], (DEPTH, 2, D_FF, D_MODEL), D_FF ** -0.5),
        "ab_w_in": nrm(ks[7], (N_EVEN, D_MODEL, AB_IN_WIDTH), D_MODEL ** -0.5),
        "pool_w": nrm(ks[8], (N_EVEN, POOL_GROUPS, POOL_GROUP_WIDTH, POOL_GROUP_WIDTH), POOL_GROUP_WIDTH ** -0.5),
        "pool_scale": 1.0 + nrm(ks[9], (N_EVEN, POOL_WIDTH), 0.02),
        "lam_q1": nrm(ks[10], (N_EVEN, DIFF_HEAD_DIM), 0.1),
        "lam_k1": nrm(ks[11], (N_EVEN, DIFF_HEAD_DIM), 0.1),
        "lam_q2": nrm(ks[12], (N_EVEN, DIFF_HEAD_DIM), 0.1),
        "lam_k2": nrm(ks[13], (N_EVEN, DIFF_HEAD_DIM), 0.1),
        "subln_g": 1.0 + nrm(ks[14], (N_EVEN, DIFF_V_DIM), 0.02),
        "ab_w_out": nrm(ks[15], (N_EVEN, MIX_WIDTH, D_MODEL), MIX_WIDTH ** -0.5),
        "rel_bias": nrm(ks[16], (REL_BUCKETS, N_HEADS_DIFF), 0.5),
        "conv_w_in": nrm(ks[17], (N_ODD, D_MODEL, 3 * D_MODEL), D_MODEL ** -0.5),
        "conv_w": nrm(ks[18], (N_ODD, CONV_WIDTH, D_MODEL), CONV_WIDTH ** -0.5),
        "conv_w_out": nrm(ks[19], (N_ODD, D_MODEL, D_MODEL), D_MODEL ** -0.5),
    }


def reference(x, ffn_norm_g, mix_norm_g, final_norm_g, ffn_w_gate, ffn_w_up, ffn_w_down,
              ab_w_in, pool_w, pool_scale, lam_q1, lam_k1, lam_q2, lam_k2, subln_g, ab_w_out,
              rel_bias, conv_w_in, conv_w, conv_w_out):
    for l in range(DEPTH):
        h = _rmsnorm(x, ffn_norm_g[l, 0])
        x = x + 0.5 * _swiglu(h, ffn_w_gate[l, 0], ffn_w_up[l, 0], ffn_w_down[l, 0])
        h = _rmsnorm(x, mix_norm_g[l])
        j = l // 2
        if l % 2 == 0:
            x = x + _even_mixer(h, l, ab_w_in[j], pool_w[j], pool_scale[j], lam_q1[j], lam_k1[j],
                                lam_q2[j], lam_k2[j], subln_g[j], ab_w_out[j], rel_bias)
        else:
            x = x + _conv_mixer(h, conv_w_in[j], conv_w[j], conv_w_out[j])
        h = _rmsnorm(x, ffn_norm_g[l, 1])
        x = x + 0.5 * _swiglu(h, ffn_w_gate[l, 1], ffn_w_up[l, 1], ffn_w_down[l, 1])
    return _rmsnorm(x, final_norm_g)
```

```python
import math
from contextlib import ExitStack

import numpy as np
import concourse.bass as bass
import concourse.mybir as mybir
from concourse.bass_utils import run_bass_kernel_spmd

F32 = mybir.dt.float32
BF16 = mybir.dt.bfloat16
AF = mybir.ActivationFunctionType
ALU = mybir.AluOpType

NORM_EPS = 1e-6
NEG_BIAS = -30000.0
REL_BUCKETS = 32
REL_MAX_DIST = 128
POOL_WINDOWS = (2, 4, 8, 16)
TN = 512
NTK = 256
MIXW = 12800
POOL_SHARE = False


def make_cfg(S=2048, D=2048, DFF=5632, DEPTH=4, FG=11):
    c = dict(S=S, D=D, DFF=DFF, DEPTH=DEPTH, FG=FG)
    c["DC"] = D // 128
    c["FC"] = DFF // 128
    c["TT"] = S // TN
    c["PC"] = D // 2 // 128
    c["GCH"] = c["PC"] // 4
    c["NH"] = (D // 2) // 256
    c["NE"] = (DEPTH + 1) // 2
    c["NO"] = DEPTH // 2
    c["NG"] = 3 * DEPTH + 1
    assert c["FC"] % FG == 0 and FG <= 16 and c["PC"] % 4 == 0
    return c


class Res:
    __slots__ = ("name", "w", "r", "sem", "semcnt")

    def __init__(self, name):
        self.name = name
        self.w = None
        self.r = {}
        self.sem = None
        self.semcnt = 0


class Multi:
    __slots__ = ("parts",)

    def __init__(self, parts):
        self.parts = tuple(parts)


def _flat(rs):
    for r in rs:
        if isinstance(r, Multi):
            yield from r.parts
        else:
            yield r


class Eng:
    def __init__(self, name, sem):
        self.name = name
        self.sem = sem
        self.cnt = 0
        self.seen = {}
        self.ops = []


class Prog:
    def __init__(self, nc, es):
        self.nc = nc
        self.es = es
        self.nsem = 0
        self.eng = {n: Eng(n, self.new_sem("tl_" + n)) for n in ("pe", "act", "dve", "pool", "sp")}
        self.all_res = []
        self.res_by_name = {}
        self.dma_last = {}
        self.swq = []
        self.bank_i = 0

    def new_sem(self, name):
        self.nsem += 1
        return self.es.enter_context(self.nc.semaphore(name))

    def res(self, name):
        r = self.res_by_name.get(name)
        if r is None:
            r = Res(name)
            self.res_by_name[name] = r
            self.all_res.append(r)
        return r

    def sb(self, name, shape, dt):
        return self.es.enter_context(self.nc.sbuf_tensor(name, shape, dt))

    def _collect(self, e, reads, writes, extra=()):
        need = {}
        reads = tuple(_flat(reads))
        writes = tuple(_flat(writes))

        def add(ev, raw=True):
            sem, val, src = ev
            if src == "pe" and e.name == "pe":
                return
            if src == e.name and not raw:
                return
            k = id(sem)
            if e.seen.get(k, 0) >= val:
                return
            if k not in need or need[k][1] < val:
                need[k] = (sem, val)

        for r in reads:
            if r.w is not None:
                add(r.w)
        for w in writes:
            if w.w is not None:
                add(w.w, raw=False)
            for ev in w.r.values():
                add(ev, raw=False)
        for ev in extra:
            add(ev)
        for k, (sem, val) in need.items():
            e.seen[k] = val
        return list(need.values())

    def _mark(self, ev, reads, writes):
        k = id(ev[0])
        reads = tuple(_flat(reads))
        writes = tuple(_flat(writes))
        for r in reads:
            old = r.r.get(k)
            if old is None or old[1] < ev[1]:
                r.r[k] = ev
        for w in writes:
            w.w = ev
            w.r = {}

    def op(self, en, fn, reads=(), writes=()):
        e = self.eng[en]
        waits = self._collect(e, reads, writes)
        e.cnt += 1
        ev = (e.sem, e.cnt, en)
        e.ops.append((waits, fn, (e.sem, 1)))
        self._mark(ev, reads, writes)
        return ev

    def dma(self, q, out_ap, in_ap, sem_res, reads=(), writes=()):
        e = self.eng[q]
        extra = ()
        if q == "pool" and len(self.swq) >= 2:
            extra = (self.swq[-2],)
        waits = self._collect(e, reads, writes, extra)
        if isinstance(sem_res, Multi):
            sem_res = sem_res.parts[0]
        if sem_res.sem is None:
            sem_res.sem = self.new_sem("d_" + sem_res.name)
        sem_res.semcnt += 16
        ev = (sem_res.sem, sem_res.semcnt, "dma")
        e.ops.append((waits, lambda h, o=out_ap, i=in_ap: h.dma_start(out=o, in_=i), (sem_res.sem, 16)))
        self._mark(ev, reads, writes)
        self.dma_last[id(sem_res.sem)] = ev
        if q == "pool":
            self.swq.append(ev)
        return ev

    def mm(self, out_ap, pairs, reads, writes):
        n = len(pairs)

        def fn(h, out_ap=out_ap, pairs=pairs, n=n):
            ins = None
            for i, (l, r) in enumerate(pairs):
                ins = h.matmul(out_ap, lhsT=l, rhs=r, start=(i == 0), stop=(i == n - 1))
            return ins

        return self.op("pe", fn, reads, writes)

    def mm1(self, out_ap, l, r, start, stop, reads, writes):
        return self.op("pe", lambda h: h.matmul(out_ap, lhsT=l, rhs=r, start=start, stop=stop,
                                                  skip_group_check=True), reads, writes)

    def barrier(self):
        evs = [(e.sem, e.cnt, e.name) for e in self.eng.values() if e.cnt > 0]
        evs += list(self.dma_last.values())
        for e in self.eng.values():
            waits = self._collect(e, (), (), evs)
            if waits:
                e.ops.append((waits, None, None))
        for r in self.all_res:
            r.w = None
            r.r = {}
        self.dma_last = {}

    def emit(self):
        nc = self.nc

        def replay(e, h):
            for waits, fn, inc in e.ops:
                for sem, val in waits:
                    h.wait_ge(sem, val)
                if fn is not None:
                    ins = fn(h)
                    if inc is not None:
                        ins.then_inc(inc[0], inc[1])

        with nc.Block() as block:
            @block.sync
            def _(h):
                replay(self.eng["sp"], h)

            @block.scalar
            def _(h):
                replay(self.eng["act"], h)

            @block.vector
            def _(h):
                replay(self.eng["dve"], h)

            @block.gpsimd
            def _(h):
                replay(self.eng["pool"], h)

            @block.tensor
            def _(h):
                replay(self.eng["pe"], h)


def t5_bucket_np(n):
    max_exact = REL_BUCKETS // 2
    nf = np.maximum(n, max_exact).astype(np.float32)
    large = max_exact + (np.log(nf / np.float32(max_exact)) / np.float32(math.log(REL_MAX_DIST / max_exact))
                         * np.float32(REL_BUCKETS - max_exact)).astype(np.int32)
    large = np.minimum(large, REL_BUCKETS - 1)
    return np.where(n < max_exact, n, large)


def build_program(cfg):
    S, D, DFF, DEPTH, FG = cfg["S"], cfg["D"], cfg["DFF"], cfg["DEPTH"], cfg["FG"]
    DC, FC, TT, PC, GCH, NH, NE, NO, NG = (cfg[k] for k in ("DC", "FC", "TT", "PC", "GCH", "NH", "NE", "NO", "NG"))
    AW = D // 2
    nc = bass.Bass("TRN2", target_bir_lowering=False)

    def din(name, shape):
        return nc.dram_tensor(name, list(shape), F32, kind="ExternalInput").ap()

    xT = din("xT", [DC, 128, S])
    yT = nc.dram_tensor("yT", [DC, 128, S], F32, kind="ExternalOutput").ap()
    w_gate = din("ffn_w_gate", [DEPTH, 2, D, DFF])
    w_up = din("ffn_w_up", [DEPTH, 2, D, DFF])
    w_down = din("ffn_w_down", [DEPTH, 2, DFF, D])
    ab_w_in = din("ab_w_in", [NE, D, 2 * D])
    pool_w = din("pool_w", [NE, 4, GCH * 128, GCH * 128])
    ab_w_out = din("ab_w_out", [NE, D, D])
    conv_w_in = din("conv_w_in", [max(NO, 1), D, 3 * D])
    conv_w_out = din("conv_w_out", [max(NO, 1), D, D])
    gv_d = din("gv", [128, NG * DC])
    psv_d = din("psv", [128, NE * PC])
    sgv_d = din("sgv", [128, NE * 2])
    lamv_d = din("lamv", [128, NE * 4])
    cwv_d = din("cwv", [128, max(NO, 1) * 3 * DC])
    rbx_d = din("rbx", [33, NH])
    oh_d = din("oh", [33, 768])
    rc16_d = din("rc16", [128, 16])
    rb_d = din("rel_bias", [REL_BUCKETS, NH])
    trep_t = nc.dram_tensor("trep", [NH, 128, 768], F32, kind="Internal")
    trep = trep_t.ap()

    es = ExitStack()
    with es:
        P = Prog(nc, es)
        hT = P.sb("hT", [128, DC, S], BF16)
        AT = P.sb("AT", [128, 16, S], BF16)
        wcol = P.sb("wcol", [128, 6, 16, 128], BF16)
        MIX = P.sb("mix", [128, MIXW], F32)
        gv = P.sb("gvs", [128, NG * DC], F32)
        psv = P.sb("psvs", [128, NE * PC], F32)
        sgv = P.sb("sgvs", [128, NE * 2], F32)
        lamv = P.sb("lamvs", [128, NE * 4], F32)
        cwv = P.sb("cwvs", [128, max(NO, 1) * 3 * DC], F32)
        rc16 = P.sb("rc16s", [128, 16], F32)
        c31 = P.sb("c31", [128, NH], F32)
        ones_b = P.sb("ones_b", [128, 128], BF16)
        ones_f = P.sb("ones_f", [128, 128], F32)
        epsc = P.sb("epsc", [128, 1], F32)
        neglam = P.sb("neglam", [128, NE], F32)
        gsub = P.sb("gsub", [128, NE * 2], F32)
        small = P.sb("small", [128, 8 * NE], F32)
        banks = [es.enter_context(nc.psum_tensor(f"bank{i}", [128, TN], F32)) for i in range(8)]
        bank_r = [P.res(f"bank{i}") for i in range(8)]

        hT_r2 = [P.res(f"hT{t}") for t in range(S // NTK)]
        RPT = TN // NTK
        hT_r = [tuple(hT_r2[t * RPT:(t + 1) * RPT]) for t in range(TT)]
        AT_r = [[P.res(f"AT{k}_{t}") for t in range(TT)] for k in range(16)]
        wcol_r = [P.res(f"wcol{i}") for i in range(6)]
        yT_r = [P.res(f"yT{c}") for c in range(DC)]
        const_r = P.res("consts")
        misc_r = P.res("misc")
        wstate = {"i": 0}
        FB = 256
        fblk = [P.res(f"mixfb{i}") for i in range(MIXW // FB)]

        def mres(off, n):
            return Multi(fblk[off // FB:(off + n - 1) // FB + 1])


        def recip(out, in_, n, reads, writes):
            P.op("dve", lambda h: h.reciprocal(out=out, in_=in_), reads=reads, writes=writes)

        def mixf(off, n):
            return MIX[:, off:off + n]

        def mixb(off, n):
            return MIX[:, off:off + n].bitcast(BF16)

        def tsl(t):
            return slice(t * TN, (t + 1) * TN)

        def next_bank(pool=range(8)):
            pool = list(pool)
            b = pool[P.bank_i % len(pool)]
            P.bank_i += 1
            return banks[b], bank_r[b]

        def next_wslot():
            i = wstate["i"] % 6
            wstate["i"] += 1
            return wcol[:, i], wcol_r[i]

        def load_cols(W2d, col0, nK=DC, width=128):
            ap, r = next_wslot()
            src = W2d.rearrange("(c p) f -> p c f", p=128)[:, :, col0:col0 + width]
            P.dma("pool", ap[:, :nK, :width], src, r, reads=(), writes=(r,))
            return ap, r

        def load_cols_pair(W2d, col0, nK=DC):
            wstate["i"] += (-wstate["i"]) % 6 + 4
            i = wstate["i"] % 6
            assert i == 4
            wstate["i"] += 2
            ap = wcol[:, 4:6].rearrange("p a c f -> p (a c f)").rearrange("p (c f) -> p c f", f=256)
            r = Multi([wcol_r[4], wcol_r[5]])
            src = W2d.rearrange("(c p) f -> p c f", p=128)[:, :, col0:col0 + 256]
            P.dma("pool", ap[:, :nK, :], src, r, reads=(), writes=(r,))
            return ap, r

        def load_rows(W2d, k0, nK, col0):
            ap, r = next_wslot()
            src = W2d[k0 * 128:(k0 + nK) * 128, col0:col0 + 128].rearrange("(k p) m -> p k m", p=128)
            P.dma("pool", ap[:, :nK, :], src, r, reads=(), writes=(r,))
            return ap, r

        tmp_r = P.res("setup_tmp")
        for nm, dst, src in (("gv", gv, gv_d), ("psv", psv, psv_d), ("sgv", sgv, sgv_d), ("lamv", lamv, lamv_d),
                             ("cwv", cwv, cwv_d), ("rc16", rc16, rc16_d)):
            P.dma("sp", dst[:], src[:, :], P.res("ld_" + nm), reads=(), writes=(const_r,))
        P.dma("sp", c31[:], bass.AP(rb_d.tensor, 31 * NH, [[0, 128], [1, NH]]), P.res("ld_c31"), writes=(const_r,))
        rbx = mixf(0, NH)[0:33, :]
        oh = mixf(64, 768)[0:33, :]
        P.dma("sp", rbx, rbx_d[:, :], P.res("ld_rbx"), writes=(tmp_r,))
        P.dma("sp", oh, oh_d[:, :], P.res("ld_oh"), writes=(tmp_r,))
        P.op("dve", lambda h: h.memset(ones_b[:], 1.0), writes=(const_r,))
        P.op("dve", lambda h: h.memset(ones_f[:], 1.0), writes=(const_r,))
        P.op("dve", lambda h: h.memset(epsc[:], NORM_EPS), writes=(const_r,))
        Lh = mixf(1024, 128)[0:33, :]
        tsb = mixf(2048, 768)
        Lh_r, tsb_r = P.res("Lh"), P.res("tsb")
        for h_ in range(NH):
            P.op("dve", lambda h, h_=h_: h.tensor_scalar(out=Lh, in0=ones_f[0:33, :], scalar1=rbx[:, h_:h_ + 1],
                                                         scalar2=None, op0=ALU.mult),
                 reads=(tmp_r, const_r), writes=(Lh_r,))
            for half in range(2):
                bk, bkr = next_bank()
                P.mm(bk[:, 0:384], [(Lh, oh[:, half * 384:(half + 1) * 384])], reads=(Lh_r, tmp_r), writes=(bkr,))
                P.op("dve", lambda h, bk=bk, half=half: h.tensor_copy(out=tsb[:, half * 384:(half + 1) * 384],
                                                                        in_=bk[:, 0:384]),
                     reads=(bkr,), writes=(tsb_r,))
            P.dma("sp", trep[h_], tsb, tsb_r, reads=(tsb_r,), writes=(misc_r,))
        lam_init = [0.8 - 0.6 * math.exp(-0.3 * (2 * j)) for j in range(NE)]
        prod_r = P.res("prod")
        for j in range(NE):
            for k in range(2):
                P.op("dve", lambda h, j=j, k=k: h.tensor_tensor(out=small[:, 2 * j + k:2 * j + k + 1],
                                                                in0=lamv[:, 4 * j + 2 * k:4 * j + 2 * k + 1],
                                                                in1=lamv[:, 4 * j + 2 * k + 1:4 * j + 2 * k + 2], op=ALU.mult),
                     reads=(const_r,), writes=(prod_r,))
        bk, bkr = next_bank()
        P.mm(bk[:, 0:2 * NE], [(ones_f[:], small[:, 0:2 * NE])], reads=(prod_r, const_r), writes=(bkr,))
        ex = small[:, 2 * NE:4 * NE]
        ex_r = P.res("ex")
        P.op("act", lambda h, bk=bk: h.activation(out=ex, in_=bk[:, 0:2 * NE], func=AF.Exp), reads=(bkr,), writes=(ex_r,))
        lam_r = P.res("lam")
        for j in range(NE):
            P.op("dve", lambda h, j=j: h.tensor_tensor(out=small[:, 4 * NE + j:4 * NE + j + 1], in0=ex[:, 2 * j:2 * j + 1],
                                                       in1=ex[:, 2 * j + 1:2 * j + 2], op=ALU.subtract),
                 reads=(ex_r,), writes=(lam_r,))
            P.op("dve", lambda h, j=j: h.tensor_scalar(out=neglam[:, j:j + 1], in0=small[:, 4 * NE + j:4 * NE + j + 1],
                                                       scalar1=lam_init[j], scalar2=-1.0, op0=ALU.add, op1=ALU.mult),
                 reads=(lam_r,), writes=(const_r,))
            P.op("dve", lambda h, j=j: h.tensor_scalar(out=gsub[:, 2 * j:2 * j + 2], in0=sgv[:, 2 * j:2 * j + 2],
                                                       scalar1=1.0 - lam_init[j], scalar2=None, op0=ALU.mult),
                 reads=(const_r,), writes=(const_r,))

        P.barrier()
        yT_p = yT.rearrange("c p s -> p c s")

        def norm_phase(gidx, final=False, src_p=None):
            src_p = yT_p if src_p is None else src_p
            nt = S // NTK
            assert DC * NTK <= 4096
            NS = 5
            assert DC * NTK <= 2 * S
            CA = DC - 4
            colsA = CA * NTK
            normx = [mixf(i * 4096, DC * NTK).rearrange("p (c s) -> p c s", c=DC) for i in range(3)]
            nx_rr = [(mres(i * 4096, colsA), mres(i * 4096 + colsA, 4 * NTK)) for i in range(3)]
            assert colsA % FB == 0 and colsA % (S // 2) == 0
            ka = colsA // (S // 2)
            for a0 in (4, 8):
                v_ = AT[:, a0:a0 + 4, :].rearrange("p a s -> p (a s)").bitcast(F32)
                normx.append(v_[:, 0:DC * NTK].rearrange("p (c s) -> p c s", c=DC))
                nx_rr.append((Multi([AT_r[k][t] for k in range(a0, a0 + ka) for t in range(TT)]),
                              Multi([AT_r[k][t] for k in range(a0 + ka, a0 + 4) for t in range(TT)])))
            rstds = [mixf(12288 + i * NTK, NTK) for i in range(2)]
            rstd_rs = [mres(12288 + i * NTK, NTK) for i in range(2)]

            def load(j):
                sl = j % NS
                P.dma("sp", normx[sl][:, 0:CA, :], src_p[:, 0:CA, j * NTK:(j + 1) * NTK], nx_rr[sl][0],
                      reads=tuple(yT_r[0:CA]), writes=(nx_rr[sl][0],))
                P.dma("sp", normx[sl][:, CA:DC, :], src_p[:, CA:DC, j * NTK:(j + 1) * NTK], nx_rr[sl][1],
                      reads=tuple(yT_r[CA:DC]), writes=(nx_rr[sl][1],))

            stat = {}

            def square(j):
                nx, nx_r = normx[j % NS], nx_rr[j % NS]
                sq = hT[:, :, j * NTK:(j + 1) * NTK]
                sq_r = hT_r2[j]
                P.op("act", lambda h, nx=nx, sq=sq: h.activation(out=sq, in_=nx, func=AF.Square), reads=nx_r, writes=(sq_r,))
                bk, bkr = next_bank()
                P.mm(bk[:, 0:NTK], [(ones_b[:], sq[:, c, :]) for c in range(DC)], reads=(sq_r, const_r), writes=(bkr,))
                stat[j] = (bk, bkr)

            for j0 in range(min(NS - 1, nt)):
                load(j0)
            square(0)
            for j in range(nt):
                sl = j % NS
                if j + NS - 1 < nt:
                    load(j + NS - 1)
                if j + 1 < nt:
                    square(j + 1)
                nx = normx[sl]
                nx_r = nx_rr[sl]
                rstd, rstd_r = rstds[j % 2], rstd_rs[j % 2]
                bk, bkr = stat.pop(j)
                P.op("act", lambda h, bk=bk, rstd=rstd: h.activation(out=rstd, in_=bk[:, 0:NTK], func=AF.Ln, bias=epsc[:, 0:1],
                                                                     scale=1.0 / D), reads=(bkr, const_r), writes=(rstd_r,))
                P.op("act", lambda h, rstd=rstd: h.activation(out=rstd, in_=rstd, func=AF.Exp, scale=-0.5),
                     reads=(rstd_r,), writes=(rstd_r,))
                for c in range(DC):
                    if final:
                        out = nx[:, c, :]
                        wr = nx_r
                    else:
                        out = hT[:, c, j * NTK:(j + 1) * NTK]
                        wr = (hT_r2[j],)
                    if POOL_SHARE and c % 4 == 3:
                        P.op("pool", lambda h, nx=nx, c=c: h.tensor_scalar(
                            out=nx[:, c, :], in0=nx[:, c, :], scalar1=gv[:, gidx * DC + c:gidx * DC + c + 1], scalar2=None,
                            op0=ALU.mult), reads=nx_r + (const_r,), writes=nx_r)
                        P.op("pool", lambda h, out=out, nx=nx, c=c, rstd=rstd: h.tensor_tensor(
                            out=out, in0=nx[:, c, :], in1=rstd, op=ALU.mult), reads=nx_r + (rstd_r,), writes=wr)
                        continue
                    P.op("dve", lambda h, out=out, nx=nx, c=c, rstd=rstd: h.scalar_tensor_tensor(
                        out=out, in0=nx[:, c, :], scalar=gv[:, gidx * DC + c:gidx * DC + c + 1], in1=rstd,
                        op0=ALU.mult, op1=ALU.mult), reads=nx_r + (rstd_r, const_r), writes=wr)
                if final:
                    P.dma("sp", yT_p[:, :, j * NTK:(j + 1) * NTK], nx, nx_rr[sl][0], reads=nx_r, writes=tuple(yT_r))

        def phase_b(W2d, k0, nK, scale, barrier=False, xsrc=None):
            xsrc = yT if xsrc is None else xsrc
            NX = 4
            assert S <= 2048
            xin = [mixf(i * 2048, S) for i in range(NX)]
            xin_r = [mres(i * 2048, S) for i in range(NX)]
            slots = {}

            def load(c):
                sl = c % NX
                P.dma("sp", xin[sl], xsrc[c], xin_r[sl], reads=(yT_r[c],), writes=(xin_r[sl],))
                slots[c] = load_rows(W2d, k0, nK, c * 128)

            load(0)
            for c in range(DC):
                if c + 1 < DC:
                    load(c + 1)
                sl = c % NX
                wap, wr = slots.pop(c)
                for t in range(TT):
                    bk, bkr = next_bank()
                    P.mm(bk[:, :], [(wap[:, k, :], AT[:, k, tsl(t)]) for k in range(nK)],
                         reads=(wr,) + tuple(AT_r[k][t] for k in range(nK)), writes=(bkr,))
                    P.op("dve", lambda h, bk=bk, sl=sl, t=t: h.scalar_tensor_tensor(
                        out=xin[sl][:, tsl(t)], in0=bk[:, :], scalar=float(scale), in1=xin[sl][:, tsl(t)],
                        op0=ALU.mult, op1=ALU.add), reads=(bkr, xin_r[sl]), writes=(xin_r[sl],))
                P.dma("sp", yT[c], xin[sl], xin_r[sl], reads=(xin_r[sl],), writes=(yT_r[c],))

        def ffn_prefetch(l, i, f0):
            return [(load_cols(w_gate[l, i], (f0 + fc) * 128), load_cols(w_up[l, i], (f0 + fc) * 128)) for fc in range(2)]

        def ffn_phase_a(l, i, f0, pre=None):
            assert 2 <= FG <= 11
            sgv_ = AT[:, 12:16, :].rearrange("p a s -> p (a s)").bitcast(F32)
            sg = [sgv_[:, k * TN:(k + 1) * TN] for k in range(3)]
            sg_r = [P.res(f"sg{k}") for k in range(3)]
            slots = {}
            cnt = {"n": 0}

            def load(fc):
                col = (f0 + fc) * 128
                slots[fc] = (load_cols(w_gate[l, i], col), load_cols(w_up[l, i], col))

            def unit(fc, t):
                (ga, gr), (ua, ur) = slots[fc]
                bg, bgr = next_bank()
                bu, bur = next_bank()
                P.mm(bg[:, :], [(ga[:, dc, :], hT[:, dc, tsl(t)]) for dc in range(DC)], reads=(gr,) + hT_r[t], writes=(bgr,))
                P.mm(bu[:, :], [(ua[:, dc, :], hT[:, dc, tsl(t)]) for dc in range(DC)], reads=(ur,) + hT_r[t], writes=(bur,))
                k = cnt["n"] % 3
                cnt["n"] += 1
                P.op("act", lambda h, bg=bg, k=k: h.activation(out=sg[k], in_=bg[:, :], func=AF.Silu),
                     reads=(bgr,), writes=(sg_r[k],))
                P.op("dve", lambda h, bu=bu, k=k, fc=fc, t=t: h.tensor_tensor(out=AT[:, fc, tsl(t)], in0=sg[k], in1=bu[:, :],
                                                                               op=ALU.mult),
                     reads=(sg_r[k], bur), writes=(AT_r[fc][t],))

            start = 0
            if pre is not None:
                slots[0], slots[1] = pre
                if FG > 2:
                    load(2)
                for t in range(TT):
                    for fc in (0, 1):
                        unit(fc, t)
                slots.pop(0)
                slots.pop(1)
                start = 2
            else:
                load(0)
            for fc in range(start, FG):
                if fc + 1 < FG:
                    load(fc + 1)
                for t in range(TT):
                    unit(fc, t)
                slots.pop(fc)

        def conv_phase_a(j):
            UO = FB - 2
            u = mixf(UO, S + 2)
            y = mixf(FB + S, S)
            gb = mixf(FB + 2 * S, S)
            tmp = [mixf(FB + 3 * S + k * TN, TN) for k in range(2)]
            u_r, y_r, gb_r = mres(UO, S + 2), mres(FB + S, S), mres(FB + 2 * S, S)
            tmp_rr = [mres(FB + 3 * S + k * TN, TN) for k in range(2)]
            P.op("dve", lambda h: h.memset(u[:, 0:2], 0.0), writes=(u_r,))
            W = conv_w_in[j]
            slots = {}

            def load(c):
                slots[c] = tuple(load_cols(W, part * D + c * 128) for part in range(3))

            load(0)
            n = 0
            for c in range(DC):
                if c + 1 < DC:
                    load(c + 1)
                (ba, br), (ca, cr), (xa, xr) = slots.pop(c)
                for t in range(TT):
                    bb, bbr = next_bank()
                    bc, bcr = next_bank()
                    bx, bxr = next_bank()
                    P.mm(bb[:, :], [(ba[:, dc, :], hT[:, dc, tsl(t)]) for dc in range(DC)], reads=(br,) + hT_r[t], writes=(bbr,))
                    P.mm(bc[:, :], [(ca[:, dc, :], hT[:, dc, tsl(t)]) for dc in range(DC)], reads=(cr,) + hT_r[t], writes=(bcr,))
                    P.mm(bx[:, :], [(xa[:, dc, :], hT[:, dc, tsl(t)]) for dc in range(DC)], reads=(xr,) + hT_r[t], writes=(bxr,))
                    k = n % 2
                    n += 1
                    P.op("act", lambda h, bb=bb, t=t: h.activation(out=gb[:, tsl(t)], in_=bb[:, :], func=AF.Copy),
                         reads=(bbr,), writes=(gb_r,))
                    P.op("act", lambda h, bc=bc, k=k: h.activation(out=tmp[k], in_=bc[:, :], func=AF.Copy),
                         reads=(bcr,), writes=(tmp_rr[k],))
                    P.op("dve", lambda h, bx=bx, k=k, t=t: h.tensor_tensor(out=u[:, 2 + t * TN:2 + (t + 1) * TN], in0=tmp[k],
                                                                            in1=bx[:, :], op=ALU.mult),
                         reads=(tmp_rr[k], bxr), writes=(u_r,))

                def cwcol(k, c=c):
                    o = (j * 3 + k) * DC + c
                    return cwv[:, o:o + 1]

                P.op("dve", lambda h, cwcol=cwcol: h.tensor_scalar(out=y, in0=u[:, 2:2 + S], scalar1=cwcol(2), scalar2=None,
                                                                   op0=ALU.mult), reads=(u_r, const_r), writes=(y_r,))
                P.op("dve", lambda h, cwcol=cwcol: h.scalar_tensor_tensor(out=y, in0=u[:, 1:1 + S], scalar=cwcol(1), in1=y,
                                                                          op0=ALU.mult, op1=ALU.add),
                     reads=(u_r, y_r), writes=(y_r,))
                P.op("dve", lambda h, cwcol=cwcol: h.scalar_tensor_tensor(out=y, in0=u[:, 0:S], scalar=cwcol(0), in1=y,
                                                                          op0=ALU.mult, op1=ALU.add),
                     reads=(u_r, y_r), writes=(y_r,))
                P.op("dve", lambda h, c=c: h.tensor_tensor(out=AT[:, c, :], in0=gb, in1=y, op=ALU.mult),
                     reads=(gb_r, y_r), writes=tuple(AT_r[c]))

        def pool_phase_a(j):
            PADW = FB
            L = PADW + S
            ub = mixf(0, L)
            pa = mixf(L, L)
            pb = mixf(2 * L, L)
            pT = mixb(3 * L, GCH * S // 2).rearrange("p (g s) -> p g s", g=GCH)
            o_pw = 3 * L + GCH * S // 2
            pw = mixb(o_pw, 4 * GCH * GCH * 128 // 2).rearrange("p (g k m) -> p g k m", g=4, k=GCH)
            t16 = mixf(o_pw + 4 * GCH * GCH * 128 // 2, 16)
            npw = 4 * GCH * GCH * 128 // 2
            ub_r, pa_r, pb_r = mres(0, L), mres(L, L), mres(2 * L, L)
            pT_r, pw_r, t16_r = mres(3 * L, GCH * S // 2), mres(o_pw, npw), mres(o_pw + npw, 16)
            for buf, r in ((ub, ub_r), (pa, pa_r), (pb, pb_r)):
                P.op("dve", lambda h, buf=buf: h.memset(buf[:, 0:PADW], 0.0), writes=(r,))
            P.dma("pool", pw, pool_w[j].rearrange("g (k p) m -> p g k m", p=128), pw_r, writes=(pw_r,))
            W = ab_w_in[j]
            slots = {}

            def load(cu):
                slots[cu] = load_cols(W, cu * 128)

            load(0)
            for g in range(4):
                win = POOL_WINDOWS[g]
                for kc in range(GCH):
                    cu = g * GCH + kc
                    if cu + 1 < PC:
                        load(cu + 1)
                    wa, wr = slots.pop(cu)
                    for t in range(TT):
                        bk, bkr = next_bank()
                        P.mm(bk[:, :], [(wa[:, dc, :], hT[:, dc, tsl(t)]) for dc in range(DC)], reads=(wr,) + hT_r[t], writes=(bkr,))
                        P.op("act", lambda h, bk=bk, t=t: h.activation(out=ub[:, PADW + t * TN:PADW + (t + 1) * TN], in_=bk[:, :],
                                                                        func=AF.Copy), reads=(bkr,), writes=(ub_r,))
                    src, src_r = ub, ub_r
                    dsts = [(pa, pa_r), (pb, pb_r)]
                    sh = 1
                    di = 0
                    while sh < win:
                        dst, dst_r = dsts[di % 2]
                        di += 1
                        P.op("dve", lambda h, dst=dst, src=src, sh=sh: h.tensor_tensor(
                            out=dst[:, PADW:L], in0=src[:, PADW:L], in1=src[:, PADW - sh:L - sh], op=ALU.add),
                            reads=(src_r,), writes=(dst_r,))
                        src, src_r = dst, dst_r
                        sh *= 2
                    P.op("dve", lambda h, src=src, kc=kc, win=win: h.scalar_tensor_tensor(
                        out=pT[:, kc, :], in0=src[:, PADW:L], scalar=1.0 / win, in1=ub[:, PADW:L], op0=ALU.mult,
                        op1=ALU.subtract), reads=(src_r, ub_r), writes=(pT_r,))
                    nf = win - 1
                    P.op("dve", lambda h, src=src, nf=nf: h.tensor_tensor(out=t16[:, 0:nf], in0=src[:, PADW:PADW + nf],
                                                                          in1=rc16[:, 0:nf], op=ALU.mult),
                         reads=(src_r, const_r), writes=(t16_r,))
                    P.op("dve", lambda h, kc=kc, nf=nf: h.tensor_tensor(out=pT[:, kc, 0:nf], in0=t16[:, 0:nf],
                                                                        in1=ub[:, PADW:PADW + nf], op=ALU.subtract),
                         reads=(t16_r, ub_r, pT_r), writes=(pT_r,))
                for m in range(GCH):
                    ch = g * GCH + m
                    for t in range(TT):
                        bk, bkr = next_bank()
                        P.mm(bk[:, :], [(pw[:, g, kc, m * 128:(m + 1) * 128], pT[:, kc, tsl(t)]) for kc in range(GCH)],
                             reads=(pw_r, pT_r), writes=(bkr,))
                        P.op("dve", lambda h, bk=bk, ch=ch, t=t: h.tensor_scalar(
                            out=AT[:, ch, tsl(t)], in0=bk[:, :], scalar1=psv[:, j * PC + ch:j * PC + ch + 1], scalar2=None,
                            op0=ALU.mult), reads=(bkr, const_r), writes=(AT_r[ch][t],))

        def attn_phase_a(j):
            o = 0
            offs = {}

            def take(name, n, align=FB):
                nonlocal o
                o = (o + align - 1) // align * align
                offs[name] = (o, n)
                o += n
                return offs[name][0]

            QT = mixb(take("QT", S), S).rearrange("p (m s) -> p m s", m=2)
            KT = mixb(take("KT", S), S).rearrange("p (m s) -> p m s", m=2)
            NB = S // 128
            V = mixb(take("V", NB * 128), NB * 128).rearrange("p (b v) -> p b v", b=NB)
            Bt = mixf(take("Bt", 640), 640)
            NPT = 3
            PT = [mixb(take(f"PT{k}", 256), 256) for k in range(NPT)]
            o0 = mixf(take("o0", 2 * TN), 2 * TN).rearrange("p (v s) -> p v s", v=2)
            att = o0
            Eb = mixf(take("Eb", 2 * TN), 2 * TN).rearrange("p (v s) -> p v s", v=2)
            sqb = mixb(take("sqb", TN), TN).rearrange("p (v s) -> p v s", v=2)
            rden = mixf(take("rden", TN), TN)
            rstd = rden
            tmpS = [mixf(take(f"tmpS{k}", TN), TN) for k in range(2)]
            assert o <= MIXW, o
            rs_ = {k: mres(*v) for k, v in offs.items()}
            QT_r, KT_r, V_r, Bt_r = rs_["QT"], rs_["KT"], rs_["V"], rs_["Bt"]
            PT_r = [rs_[f"PT{k}"] for k in range(NPT)]
            o0_r, sqb_r, rden_r = rs_["o0"], rs_["sqb"], rs_["rden"]
            Eb_rs = [mres(offs["Eb"][0] + vh * TN, TN) for vh in range(2)]
            att_r = o0_r
            rstd_r = rden_r
            tmpS_r = [rs_[f"tmpS{k}"] for k in range(2)]
            W = ab_w_in[j]
            qscale = 128.0 ** -0.5
            pend_q = []
            for hd in range(NH):
                qcol = AW + hd * 256
                kcol = AW + AW + hd * 256
                vcol = AW + 2 * AW + hd * 256
                P.dma("sp", Bt, bass.AP(trep_t, hd * 128 * 768 + 127, [[767, 128], [1, 640]]), Bt_r,
                      reads=(misc_r,), writes=(Bt_r,))
                wstate["i"] += (-wstate["i"]) % 6
                wq = [load_cols(W, qcol + m * 128) for m in range(2)]
                wk = [load_cols(W, kcol + m * 128) for m in range(2)]
                wv = load_cols_pair(W, vcol)
                for m in range(2):
                    for t in range(TT):
                        bk, bkr = next_bank(range(3, 8))
                        P.mm(bk[:, :], [(wq[m][0][:, dc, :], hT[:, dc, tsl(t)]) for dc in range(DC)],
                             reads=(wq[m][1],) + hT_r[t], writes=(bkr,))
                        P.op("act", lambda h, bk=bk, m=m, t=t: h.mul(out=QT[:, m, tsl(t)], in_=bk[:, :], mul=qscale),
                             reads=(bkr,), writes=(QT_r,))
                        bk, bkr = next_bank(range(3, 8))
                        P.mm(bk[:, :], [(wk[m][0][:, dc, :], hT[:, dc, tsl(t)]) for dc in range(DC)],
                             reads=(wk[m][1],) + hT_r[t], writes=(bkr,))
                        P.op("dve", lambda h, bk=bk, m=m, t=t: h.tensor_copy(out=KT[:, m, tsl(t)], in_=bk[:, :]),
                             reads=(bkr,), writes=(KT_r,))
                while pend_q:
                    subln(*pend_q.pop(0))
                for tb in range(NB):
                    bk, bkr = next_bank(range(3, 8))
                    P.mm(bk[:, 0:256],
                         [(hT[:, dc, tb * 128:(tb + 1) * 128], wv[0][:, dc, :]) for dc in range(DC)],
                         reads=(wv[1], hT_r2[(tb * 128) // NTK]), writes=(bkr,))
                    P.op("act", lambda h, bk=bk, tb=tb: h.activation(out=V[:, tb, :], in_=bk[:, 0:256], func=AF.Copy),
                         reads=(bkr,), writes=(V_r,))
                LOOK = 3
                SBANKS = (3, 4, 5, 7)
                blocks = [(qt, m, kb) for qt in range(TT) for m in range(2) for kb in range(4 * qt + 4)]

                def geom(qt, kb):
                    delta = qt * TN - kb * 128
                    off = 0 if delta >= 0 else -delta
                    return delta, off, TN - off

                sc = {}

                def score(idx):
                    qt, m, kb = blocks[idx]
                    delta, off, N = geom(qt, kb)
                    b = SBANKS[idx % len(SBANKS)]
                    sb_, sbr = banks[b], bank_r[b]
                    P.mm(sb_[:, 0:N], [(KT[:, m, kb * 128:(kb + 1) * 128], QT[:, m, qt * TN + off:(qt + 1) * TN])],
                         reads=(KT_r, QT_r), writes=(sbr,))
                    sc[idx] = (sb_, sbr)

                def subln(hd, qt, nbi):
                    P.op("act", lambda h: h.activation(out=sqb, in_=att, func=AF.Square), reads=(att_r,), writes=(sqb_r,))
                    nb, nbr = banks[nbi], bank_r[nbi]
                    P.mm(nb[:, :], [(ones_b[:], sqb[:, vh, :]) for vh in range(2)], reads=(sqb_r, const_r), writes=(nbr,))
                    P.op("act", lambda h, nb=nb: h.activation(out=rstd, in_=nb[:, :], func=AF.Ln, bias=epsc[:, 0:1], scale=1.0 / 256.0),
                         reads=(nbr, const_r), writes=(rstd_r,))
                    P.op("act", lambda h: h.activation(out=rstd, in_=rstd, func=AF.Exp, scale=-0.5), reads=(rstd_r,), writes=(rstd_r,))
                    for vh in range(2):
                        ch = PC + hd * 2 + vh
                        P.op("dve", lambda h, vh=vh, ch=ch, qt=qt: h.scalar_tensor_tensor(
                            out=AT[:, ch, tsl(qt)], in0=att[:, vh, :], scalar=gsub[:, 2 * j + vh:2 * j + vh + 1], in1=rstd,
                            op0=ALU.mult, op1=ALU.add if False else ALU.mult), reads=(att_r, rstd_r, const_r),
                            writes=(AT_r[ch][qt],))

                pend_age = 0
                ci = 0
                for idx in range(min(LOOK, len(blocks))):
                    score(idx)
                tsi = {"n": 0}

                def expo(idx):
                    qt, m, kb = blocks[idx]
                    delta, off, N = geom(qt, kb)
                    sb_, sbr = sc.pop(idx)
                    pk = idx % NPT
                    if delta >= 256:
                        P.op("act", lambda h, sb_=sb_, pk=pk, N=N, hd=hd: h.activation(
                            out=PT[pk][:, 0:N], in_=sb_[:, 0:N], func=AF.Exp, bias=c31[:, hd:hd + 1], scale=1.0),
                            reads=(sbr, const_r), writes=(PT_r[pk],))
                    else:
                        boff = 128 if delta == 128 else 0
                        sk = tsi["n"] % 2
                        tsi["n"] += 1
                        P.op("dve", lambda h, sb_=sb_, sk=sk, N=N, boff=boff: h.tensor_tensor(
                            out=tmpS[sk][:, 0:N], in0=sb_[:, 0:N], in1=Bt[:, boff:boff + N], op=ALU.add),
                            reads=(sbr, Bt_r), writes=(tmpS_r[sk],))
                        P.op("act", lambda h, sk=sk, pk=pk, N=N: h.activation(
                            out=PT[pk][:, 0:N], in_=tmpS[sk][:, 0:N], func=AF.Exp),
                            reads=(tmpS_r[sk],), writes=(PT_r[pk],))

                expo(0)
                for idx, (qt, m, kb) in enumerate(blocks):
                    nkb = 4 * qt + 4
                    denb = (2, 6)[ci % 2]
                    acc = [(banks[b], bank_r[b]) for b in (0, 1, denb)]
                    delta, off, N = geom(qt, kb)
                    if idx + LOOK < len(blocks):
                        score(idx + LOOK)
                    if idx + 1 < len(blocks):
                        expo(idx + 1)
                    pk = idx % NPT
                    first, last = kb == 0, kb == nkb - 1
                    for vh in range(2):
                        P.mm1(acc[vh][0][:, off:TN], V[:, kb, vh * 128:(vh + 1) * 128], PT[pk][:, 0:N], first, last,
                              reads=(V_r, PT_r[pk]), writes=(acc[vh][1],))
                    P.mm1(acc[2][0][:, off:TN], ones_b[:], PT[pk][:, 0:N], first, last,
                          reads=(const_r, PT_r[pk]), writes=(acc[2][1],))
                    if pend_q:
                        pend_age += 1
                        if pend_age >= 4:
                            subln(*pend_q.pop(0))
                            pend_age = 0
                    if not last:
                        continue
                    P.op("dve", lambda h: h.tensor_copy(out=Eb[:, 0, :], in_=banks[0][:, :]), reads=(bank_r[0],), writes=(Eb_rs[0],))
                    P.op("act", lambda h: h.activation(out=Eb[:, 1, :], in_=banks[1][:, :], func=AF.Copy),
                         reads=(bank_r[1],), writes=(Eb_rs[1],))
                    P.op("act", lambda h, denb=denb: h.activation(out=rden, in_=banks[denb][:, :], func=AF.Ln),
                         reads=(bank_r[denb],), writes=(rden_r,))
                    ci += 1
                    P.op("act", lambda h: h.activation(out=rden, in_=rden, func=AF.Exp, scale=-1.0), reads=(rden_r,), writes=(rden_r,))
                    for vh in range(2):
                        if m == 0:
                            P.op("dve", lambda h, vh=vh: h.tensor_tensor(out=o0[:, vh, :], in0=Eb[:, vh, :], in1=rden, op=ALU.mult),
                                 reads=(Eb_rs[vh], rden_r), writes=(o0_r,))
                        else:
                            P.op("dve", lambda h, vh=vh: h.tensor_tensor(out=Eb[:, vh, :], in0=Eb[:, vh, :], in1=rden, op=ALU.mult),
                                 reads=(Eb_rs[vh], rden_r), writes=(Eb_rs[vh],))
                            P.op("dve", lambda h, vh=vh: h.scalar_tensor_tensor(
                                out=att[:, vh, :], in0=Eb[:, vh, :], scalar=neglam[:, j:j + 1], in1=o0[:, vh, :],
                                op0=ALU.mult, op1=ALU.add), reads=(Eb_rs[vh], o0_r, const_r), writes=(att_r,))
                    if m == 0:
                        continue
                    pend_q.append((hd, qt, denb))
                    pend_age = 0
            while pend_q:
                subln(*pend_q.pop(0))

        xT_p = xT.rearrange("c p s -> p c s")

        def ffn(l, i):
            first = (l == 0 and i == 0)
            pre = ffn_prefetch(l, i, 0)
            norm_phase(3 * l + (0 if i == 0 else 2), src_p=xT_p if first else None)
            for f0 in range(0, FC, FG):
                ffn_phase_a(l, i, f0, pre if f0 == 0 else None)
                phase_b(w_down[l, i], f0, FG, 0.5, xsrc=xT if (first and f0 == 0) else None)

        for l in range(DEPTH):
            ffn(l, 0)
            norm_phase(3 * l + 1)
            j = l // 2
            if l % 2 == 0:
                pool_phase_a(j)
                attn_phase_a(j)
                phase_b(ab_w_out[j], 0, DC, 1.0, barrier=True)
            else:
                conv_phase_a(j)
                phase_b(conv_w_out[j], 0, DC, 1.0, barrier=True)
            ffn(l, 1)
        norm_phase(3 * DEPTH, final=True)
        P.barrier()
        P.emit()
    return nc


def host_inputs(cfg, inp):
    S, D, DEPTH = cfg["S"], cfg["D"], cfg["DEPTH"]
    DC, PC, NH, NE, NO = cfg["DC"], cfg["PC"], cfg["NH"], cfg["NE"], cfg["NO"]
    f = lambda a: np.ascontiguousarray(np.asarray(a, dtype=np.float32))

    def cols(v):
        v = np.asarray(v, dtype=np.float32)
        n = v.shape[-1] // 128
        v2 = v.reshape(-1, n, 128)
        return f(v2.transpose(2, 0, 1).reshape(128, -1))

    glist = []
    for l in range(DEPTH):
        glist += [inp["ffn_norm_g"][l, 0], inp["mix_norm_g"][l], inp["ffn_norm_g"][l, 1]]
    glist.append(inp["final_norm_g"])
    shared = {
        "ffn_w_gate": f(inp["ffn_w_gate"]), "ffn_w_up": f(inp["ffn_w_up"]), "ffn_w_down": f(inp["ffn_w_down"]),
        "ab_w_in": f(inp["ab_w_in"]), "pool_w": f(inp["pool_w"]), "ab_w_out": f(inp["ab_w_out"]),
        "conv_w_in": f(inp["conv_w_in"]), "conv_w_out": f(inp["conv_w_out"]),
        "gv": cols(np.stack([np.asarray(g) for g in glist])),
        "psv": cols(inp["pool_scale"]),
        "sgv": cols(inp["subln_g"]),
        "lamv": f(np.stack([np.asarray(inp[k]) for k in ("lam_q1", "lam_k1", "lam_q2", "lam_k2")], axis=1)
                  .reshape(NE * 4, 128).T),
        "cwv": cols(np.asarray(inp["conv_w"]).reshape(NO * 3, D)),
        "rbx": f(np.concatenate([np.asarray(inp["rel_bias"], dtype=np.float32),
                                 np.full((1, NH), NEG_BIAS, np.float32)], axis=0)),
        "rel_bias": f(inp["rel_bias"]),
    }
    dist = np.arange(768, dtype=np.int32) - 127
    bucket = np.where(dist < 0, 32, t5_bucket_np(np.maximum(dist, 0)))
    shared["oh"] = f((bucket[None, :] == np.arange(33)[:, None]).astype(np.float32))
    shared["rc16"] = f(np.broadcast_to(1.0 / np.arange(1, 17, dtype=np.float32), (128, 16)))
    x = np.asarray(inp["x"], dtype=np.float32)
    maps = []
    for b in range(x.shape[0]):
        m = dict(shared)
        m["xT"] = f(x[b].T.reshape(DC, 128, S))
        maps.append(m)
    return maps


_CACHE = {}


def run(cfg, inp):
    key = tuple(sorted(cfg.items()))
    maps = host_inputs(cfg, inp)
    nc = build_program(cfg)
    res = run_bass_kernel_spmd(nc, maps, core_ids=list(range(len(maps))))
    S, D = cfg["S"], cfg["D"]
    out = np.stack([np.asarray(r["yT"]).reshape(D, S).T for r in res.results], axis=0)
    return np.ascontiguousarray(out.astype(np.float32))


def kernel(**inputs):
    cfg = make_cfg()
    return run(cfg, inputs)
```
